# Optimizing a Trainium2 kernel written in Bass

```python
import jax
import jax.numpy as jnp
from jax import lax
import numpy as np

D_MODEL = 1024
BATCH = 1
SEQ = 16384
DEPTH = 2
DEC_BATCH = 8
DEC_SEQ = 32
PAST_LEN = 2048

CHUNK = 64
Q_BLOCK = 128
HEAD_DIM = 64
D_BRANCH = 512
N_HEADS_A = D_BRANCH // HEAD_DIM
N_IDX_HEADS = 8
IDX_DIM = 64
TOPK_MAX = 256
N_HEADS_B = D_BRANCH // HEAD_DIM
CONV_W = 3
D_FF = 2816
N_BRANCH = 3
N_IN = (3 * D_BRANCH + N_IDX_HEADS * IDX_DIM + IDX_DIM + N_IDX_HEADS
        + 3 * D_BRANCH + N_HEADS_B + 3 * D_BRANCH + N_BRANCH * D_MODEL)
EPS = 1e-6
NEG_INF = -1e30

kernel_name = 'chunk_causal_hybrid_dsa_fox_conv_step'


def _rmsnorm(x, g):
    xf = x.astype(jnp.float32)
    y = xf * lax.rsqrt(jnp.mean(xf * xf, axis=-1, keepdims=True) + EPS)
    return (y * g.astype(jnp.float32)).astype(x.dtype)


def _causal_dwconv(u, w, past):
    T = u.shape[1]
    full = jnp.concatenate([past.astype(u.dtype), u], axis=1)
    y = sum(full[:, k:k + T] * w[k] for k in range(CONV_W))
    return y, full[:, -(CONV_W - 1):]


def _in_split_points():
    sizes = (D_BRANCH, D_BRANCH, D_BRANCH, N_IDX_HEADS * IDX_DIM, IDX_DIM, N_IDX_HEADS,
             D_BRANCH, D_BRANCH, D_BRANCH, N_HEADS_B,
             D_BRANCH, D_BRANCH, D_BRANCH)
    points, acc = [], 0
    for s in sizes:
        acc += s
        points.append(acc)
    return points


def _sweep_queries(block_fn, q_inputs):
    T = q_inputs[0].shape[1]
    if T % Q_BLOCK != 0 or T <= Q_BLOCK:
        return block_fn(0, *q_inputs)
    nb = T // Q_BLOCK

    def to_blocks(a):
        a = a.reshape(a.shape[0], nb, Q_BLOCK, *a.shape[2:])
        return jnp.moveaxis(a, 1, 0)

    starts = jnp.arange(nb, dtype=jnp.int32) * Q_BLOCK
    out = lax.map(lambda args: block_fn(args[0], *args[1:]),
                  (starts, *[to_blocks(a) for a in q_inputs]))
    out = jnp.moveaxis(out, 0, 1)
    return out.reshape(out.shape[0], T, *out.shape[3:])


def _dsa_attend(q, qi, wi, k, v, ki, past_len):
    L = k.shape[1]
    n_sel = min(TOPK_MAX, L // 4)
    kpos = jnp.arange(L, dtype=jnp.int32)
    scale = HEAD_DIM ** -0.5

    def block(start, qb, qib, wib):
        nq = qb.shape[1]
        qpos = past_len + start + jnp.arange(nq, dtype=jnp.int32)
        chunk_end = (qpos // CHUNK + 1) * CHUNK
        dots = jnp.einsum('bqhe,bse->bqhs', qib, ki).astype(jnp.float32)
        score = jnp.einsum('bqhs,bqh->bqs', jax.nn.relu(dots), wib.astype(jnp.float32))
        score = jnp.where(kpos[None, None, :] < chunk_end[None, :, None], score, -jnp.inf)
        _, idx = lax.top_k(score, n_sel)
        valid = idx < chunk_end[None, :, None]
        k_sel = jax.vmap(lambda kk, ii: kk[ii])(k, idx)
        v_sel = jax.vmap(lambda vv, ii: vv[ii])(v, idx)
        logits = jnp.einsum('bqhd,bqkhd->bhqk', qb, k_sel).astype(jnp.float32) * scale
        logits = jnp.where(valid[:, None], logits, NEG_INF)
        p = jax.nn.softmax(logits, axis=-1).astype(v.dtype)
        return jnp.einsum('bhqk,bqkhd->bqhd', p, v_sel)

    return _sweep_queries(block, (q, qi, wi))


def _fox_attend(q, k, v, logf_all, past_len):
    L = k.shape[1]
    cum = jnp.cumsum(logf_all.astype(jnp.float32), axis=1)
    cum_k = jnp.moveaxis(cum, 1, 2)
    cum_q = cum[:, past_len:]
    kpos = jnp.arange(L, dtype=jnp.int32)
    scale = HEAD_DIM ** -0.5

    def block(start, qb, cq):
        nq = qb.shape[1]
        qpos = past_len + start + jnp.arange(nq, dtype=jnp.int32)
        logits = jnp.einsum('bqhd,bshd->bhqs', qb, k).astype(jnp.float32) * scale
        logits = logits + jnp.moveaxis(cq, 1, 2)[..., None] - cum_k[:, :, None, :]
        logits = jnp.where(kpos[None, None, None, :] <= qpos[None, None, :, None], logits, NEG_INF)
        p = jax.nn.softmax(logits, axis=-1).astype(v.dtype)
        return jnp.einsum('bhqs,bshd->bqhd', p, v)

    return _sweep_queries(block, (q, cum_q))


def _layer(x, c, past_idx_k, past_dsa_k, past_dsa_v, past_fox_k, past_fox_v, past_fox_logf,
           past_conv_mix, past_conv_ffn,
           w_ada, b_ada, norm_g, w_in, b_forget, conv_mix_w, w_branch, w_out, w_up,
           conv_ffn_w, w_down):
    B, T, _ = x.shape
    P = past_dsa_k.shape[1]
    mod = jnp.dot(jax.nn.silu(c), w_ada) + b_ada
    sh1, sc1, g1, sh2, sc2, g2 = [m[:, None, :] for m in jnp.split(mod, 6, axis=-1)]

    h = _rmsnorm(x, norm_g[0]) * (1 + sc1) + sh1
    z = h @ w_in
    (qa, ka, va, qi, ki, wi, qb, kb, vb, fl, cb, cc, cx, gl) = jnp.split(z, _in_split_points(), axis=-1)
    qa = qa.reshape(B, T, N_HEADS_A, HEAD_DIM)
    ka = ka.reshape(B, T, N_HEADS_A, HEAD_DIM)
    va = va.reshape(B, T, N_HEADS_A, HEAD_DIM)
    qi = qi.reshape(B, T, N_IDX_HEADS, IDX_DIM)
    qb = qb.reshape(B, T, N_HEADS_B, HEAD_DIM)
    kb = kb.reshape(B, T, N_HEADS_B, HEAD_DIM)
    vb = vb.reshape(B, T, N_HEADS_B, HEAD_DIM)

    ki_all = jnp.concatenate([past_idx_k.astype(ki.dtype), ki], axis=1)
    ka_all = jnp.concatenate([past_dsa_k.astype(ka.dtype), ka], axis=1)
    va_all = jnp.concatenate([past_dsa_v.astype(va.dtype), va], axis=1)
    ya = _dsa_attend(qa, qi, wi, ka_all, va_all, ki_all, P).reshape(B, T, D_BRANCH)

    logf = jax.nn.log_sigmoid((fl + b_forget).astype(jnp.float32))
    logf_all = jnp.concatenate([past_fox_logf.astype(jnp.float32), logf], axis=1)
    kb_all = jnp.concatenate([past_fox_k.astype(kb.dtype), kb], axis=1)
    vb_all = jnp.concatenate([past_fox_v.astype(vb.dtype), vb], axis=1)
    yb = _fox_attend(qb, kb_all, vb_all, logf_all, P).reshape(B, T, D_BRANCH)

    conv_out, new_conv_mix = _causal_dwconv(cc * cx, conv_mix_w, past_conv_mix)
    yc = cb * conv_out

    br = jnp.einsum('btnc,ncd->btnd', jnp.stack([ya, yb, yc], axis=2), w_branch)
    gates = jax.nn.sigmoid(gl.reshape(B, T, N_BRANCH, D_MODEL))
    mix = jnp.sum(gates * br, axis=2) @ w_out
    x = x + g1 * _rmsnorm(mix, norm_g[1])

    h2 = _rmsnorm(x, norm_g[2]) * (1 + sc2) + sh2
    ug, uv = jnp.split(h2 @ w_up, 2, axis=-1)
    ug_c, new_conv_ffn = _causal_dwconv(ug, conv_ffn_w, past_conv_ffn)
    f = (jax.nn.silu(ug_c) * uv) @ w_down
    x = x + g2 * _rmsnorm(f, norm_g[3])
    return x, (ki, ka, va, kb, vb, logf, new_conv_mix, new_conv_ffn)


def _stack_layers(states):
    return [jnp.stack([st[i] for st in states], axis=0) for i in range(len(states[0]))]


def setup_inputs(seed: int = 0) -> dict:
    key = jax.random.key(seed)
    ks = jax.random.split(key, 24)
    f32 = jnp.float32

    def nrm(k, shape, s):
        return jax.random.normal(k, shape, f32) * s

    return {
        'x_prompt': nrm(ks[0], (BATCH, SEQ, D_MODEL), 1.0),
        'x_sample': nrm(ks[1], (DEC_BATCH, DEC_SEQ, D_MODEL), 1.0),
        'c_prompt': nrm(ks[2], (BATCH, D_MODEL), 1.0),
        'c_sample': nrm(ks[3], (DEC_BATCH, D_MODEL), 1.0),
        'cache_idx_k': nrm(ks[4], (DEPTH, DEC_BATCH, PAST_LEN, IDX_DIM), 1.0),
        'cache_dsa_k': nrm(ks[5], (DEPTH, DEC_BATCH, PAST_LEN, N_HEADS_A, HEAD_DIM), 1.0),
        'cache_dsa_v': nrm(ks[6], (DEPTH, DEC_BATCH, PAST_LEN, N_HEADS_A, HEAD_DIM), 1.0),
        'cache_fox_k': nrm(ks[7], (DEPTH, DEC_BATCH, PAST_LEN, N_HEADS_B, HEAD_DIM), 1.0),
        'cache_fox_v': nrm(ks[8], (DEPTH, DEC_BATCH, PAST_LEN, N_HEADS_B, HEAD_DIM), 1.0),
        'cache_fox_logf': jax.nn.log_sigmoid(nrm(ks[9], (DEPTH, DEC_BATCH, PAST_LEN, N_HEADS_B), 1.0) + 1.0),
        'state_conv_mix': nrm(ks[10], (DEPTH, DEC_BATCH, CONV_W - 1, D_BRANCH), 1.0),
        'state_conv_ffn': nrm(ks[11], (DEPTH, DEC_BATCH, CONV_W - 1, D_FF), 1.0),
        'w_ada': nrm(ks[12], (DEPTH, D_MODEL, 6 * D_MODEL), 0.5 * D_MODEL ** -0.5),
        'b_ada': nrm(ks[13], (DEPTH, 6 * D_MODEL), 0.01),
        'norm_g': 1.0 + nrm(ks[14], (DEPTH, 4, D_MODEL), 0.1),
        'w_in': nrm(ks[15], (DEPTH, D_MODEL, N_IN), D_MODEL ** -0.5),
        'b_forget': 1.0 + nrm(ks[16], (DEPTH, N_HEADS_B), 0.5),
        'conv_mix_w': nrm(ks[17], (DEPTH, CONV_W, D_BRANCH), CONV_W ** -0.5),
        'w_branch': nrm(ks[18], (DEPTH, N_BRANCH, D_BRANCH, D_MODEL), D_BRANCH ** -0.5),
        'w_out': nrm(ks[19], (DEPTH, D_MODEL, D_MODEL), D_MODEL ** -0.5),
        'w_up': nrm(ks[20], (DEPTH, D_MODEL, 2 * D_FF), D_MODEL ** -0.5),
        'conv_ffn_w': nrm(ks[21], (DEPTH, CONV_W, D_FF), CONV_W ** -0.5),
        'w_down': nrm(ks[22], (DEPTH, D_FF, D_MODEL), D_FF ** -0.5),
    }


def reference(x_prompt, x_sample, c_prompt, c_sample, cache_idx_k, cache_dsa_k, cache_dsa_v,
              cache_fox_k, cache_fox_v, cache_fox_logf, state_conv_mix, state_conv_ffn,
              w_ada, b_ada, norm_g, w_in, b_forget, conv_mix_w, w_branch, w_out, w_up,
              conv_ffn_w, w_down):
    B = x_prompt.shape[0]
    dt = x_prompt.dtype
    e_idx_k = jnp.zeros((B, 0, IDX_DIM), dt)
    e_dsa = jnp.zeros((B, 0, N_HEADS_A, HEAD_DIM), dt)
    e_fox = jnp.zeros((B, 0, N_HEADS_B, HEAD_DIM), dt)
    e_logf = jnp.zeros((B, 0, N_HEADS_B), jnp.float32)
    z_conv_mix = jnp.zeros((B, CONV_W - 1, D_BRANCH), dt)
    z_conv_ffn = jnp.zeros((B, CONV_W - 1, D_FF), dt)

    yp, ys = x_prompt, x_sample
    p_states, s_states = [], []
    for l in range(DEPTH):
        lw = (w_ada[l], b_ada[l], norm_g[l], w_in[l], b_forget[l], conv_mix_w[l],
              w_branch[l], w_out[l], w_up[l], conv_ffn_w[l], w_down[l])
        yp, st_p = _layer(yp, c_prompt, e_idx_k, e_dsa, e_dsa, e_fox, e_fox, e_logf,
                          z_conv_mix, z_conv_ffn, *lw)
        ys, st_s = _layer(ys, c_sample, cache_idx_k[l], cache_dsa_k[l], cache_dsa_v[l],
                          cache_fox_k[l], cache_fox_v[l], cache_fox_logf[l],
                          state_conv_mix[l], state_conv_ffn[l], *lw)
        p_states.append(st_p)
        s_states.append(st_s)

    (p_idx_k, p_dsa_k, p_dsa_v, p_fox_k, p_fox_v, p_fox_logf,
     p_conv_mix, p_conv_ffn) = _stack_layers(p_states)
    (s_idx_k, s_dsa_k, s_dsa_v, s_fox_k, s_fox_v, s_fox_logf,
     s_conv_mix, s_conv_ffn) = _stack_layers(s_states)
    return (yp, ys, p_idx_k, p_dsa_k, p_dsa_v, p_fox_k, p_fox_v, p_fox_logf, p_conv_mix, p_conv_ffn,
            s_idx_k, s_dsa_k, s_dsa_v, s_fox_k, s_fox_v, s_fox_logf, s_conv_mix, s_conv_ffn)
```

```python
import numpy as np
from contextlib import ExitStack
import concourse.bass as bass
import concourse.mybir as mybir
from concourse.bass_utils import run_bass_kernel_spmd

F32 = mybir.dt.float32
BF16 = mybir.dt.bfloat16
AF = mybir.ActivationFunctionType
ALU = mybir.AluOpType
AX = mybir.AxisListType

NCORES = 8
D = 1024
NIN = 8272
DFF = 2816
NTOK = 2080
NRG = 17
NB_BISECT = 24
EPS = 1e-6
NDS = 12

C_QA, C_KA, C_VA, C_QI, C_KI, C_WI, C_QB, C_KB, C_VB, C_FL, C_CB, C_CC, C_CX, C_GL = (
    0, 512, 1024, 1536, 2048, 2112, 2120, 2632, 3144, 3656, 3664, 4176, 4688, 5200)


def rg_info(i):
    rows = 128 if i < 16 else 32
    return rows, i * 128, (0 if i < 16 else 1)


class Eng:
    def __init__(self, name, e, sem):
        self.name, self.e, self.sem, self.cnt, self.waited = name, e, sem, 0, {}
        self.pending = False


class Res:
    __slots__ = ("w", "r")

    def __init__(self):
        self.w = None
        self.r = {}


class TL:
    def __init__(self, t):
        self.t = t
        self.res = Res()

    def __getitem__(self, k):
        return self.t[k]


class KB:
    def __init__(self, nc, stack):
        self.nc = nc
        self.stack = stack
        self.nsem = 0
        mk = self.mksem
        self.pe = Eng("pe", nc.tensor, mk())
        self.act = Eng("act", nc.scalar, mk())
        self.dve = Eng("dve", nc.vector, mk())
        self.pool = Eng("pool", nc.gpsimd, mk())
        self.sp = Eng("sp", nc.sync, mk())
        self.engs = [self.pe, self.act, self.dve, self.pool, self.sp]
        self.dsems = {"sp": [mk() for _ in range(NDS)], "pool": [mk() for _ in range(NDS)]}
        self.dval = {"sp": [0] * NDS, "pool": [0] * NDS}
        self.dnext = {"sp": 0, "pool": 0}
        self.semkey = {}
        self.clocks = {}

    def mksem(self):
        self.nsem += 1
        return self.stack.enter_context(self.nc.semaphore("sm%d" % self.nsem))

    def _wait(self, E, tok):
        sem, val, key, owner = tok
        if owner is E and E is self.pe:
            return
        if E.waited.get(key, 0) >= val:
            return
        E.e.wait_ge(sem, val)
        E.waited[key] = val
        snap = self.clocks.get((key, val))
        if snap:
            for k2, v2 in snap.items():
                if E.waited.get(k2, 0) < v2:
                    E.waited[k2] = v2

    def _deps(self, E, reads, writes):
        need = {}

        def add(t):
            if t[3] is E and E is self.pe:
                return
            cur = need.get(t[2])
            if cur is None or cur[1] < t[1]:
                need[t[2]] = t
        for r in reads:
            if r.w is not None:
                add(r.w)
        for w in writes:
            if w.w is not None:
                add(w.w)
            for t in w.r.values():
                add(t)
        for t in need.values():
            self._wait(E, t)

    def _commit(self, tok, reads, writes):
        for r in reads:
            r.r[tok[2]] = tok
        for w in writes:
            w.w = tok
            w.r = {}

    def op(self, E, fn, reads=(), writes=(), sig=True):
        reads = [x.res if isinstance(x, TL) else x for x in reads]
        writes = [x.res if isinstance(x, TL) else x for x in writes]
        self._deps(E, reads, writes)
        ins = fn()
        if sig:
            E.cnt += 1
            ins.then_inc(E.sem, 1)
            E.pending = False
            self.clocks[(E.name, E.cnt)] = dict(E.waited)
            self._commit((E.sem, E.cnt, E.name, E), reads, writes)
        else:
            E.pending = True
            self._commit((E.sem, E.cnt + 1, E.name, E), reads, writes)

    def dma(self, Q, out, in_, reads=(), writes=(), **kw):
        reads = [x.res if isinstance(x, TL) else x for x in reads]
        writes = [x.res if isinstance(x, TL) else x for x in writes]
        self._deps(Q, reads, writes)
        i = self.dnext[Q.name]
        self.dnext[Q.name] = (i + 1) % NDS
        sem = self.dsems[Q.name][i]
        key = "d%s%d" % (Q.name, i)
        if self.dval[Q.name][i] > 0:
            self._wait(Q, (sem, self.dval[Q.name][i], key, None))
        ins = Q.e.dma_start(out=out, in_=in_, **kw)
        self.dval[Q.name][i] += 16
        ins.then_inc(sem, 16)
        self.clocks[(key, self.dval[Q.name][i])] = dict(Q.waited)
        self._commit((sem, self.dval[Q.name][i], key, None), reads, writes)

    def barrier(self):
        for E in self.engs:
            assert not E.pending, "unsignalled op pending on %s" % E.name
        toks = [(E.sem, E.cnt, E.name, E) for E in self.engs if E.cnt > 0]
        for q in ("sp", "pool"):
            for i in range(NDS):
                if self.dval[q][i] > 0:
                    toks.append((self.dsems[q][i], self.dval[q][i], "d%s%d" % (q, i), None))
        for E in self.engs:
            for t in toks:
                if t[3] is E and E is self.pe:
                    E.e.wait_ge(t[0], t[1])
                    continue
                self._wait(E, t)


def build_program():
    nc = bass.Bass("TRN2", target_bir_lowering=False)

    def din(name, shape, dt=F32):
        return nc.dram_tensor(name, list(shape), dt, kind="ExternalInput").ap()

    def dout(name, shape, dt=F32):
        return nc.dram_tensor(name, list(shape), dt, kind="ExternalOutput").ap()

    def dscr(name, shape, dt=F32):
        return nc.dram_tensor(name, list(shape), dt, kind="Internal").ap()

    x_in = din("x_in", [NTOK, D])
    cvecT = din("cvecT", [128, 8, 2])
    w_ada = din("w_ada", [2, D, 6 * D])
    b_ada = din("b_ada", [2, 6 * D])
    norm_g = din("norm_g", [2, 4 * D])
    w_in = din("w_in", [2, D, NIN])
    b_forget = din("b_forget", [2, 8])
    conv_mix_w = din("conv_mix_w", [2, 3 * 512])
    w_branch = din("w_branch", [2, 1536, D])
    w_out = din("w_out", [2, D, D])
    w_up = din("w_up", [2, D, 2 * DFF])
    conv_ffn_wT = din("conv_ffn_wT", [2, 128, 22, 3])
    w_down = din("w_down", [2, DFF, D])
    kc_T = din("kc_T", [2, 1088, 2048])
    vc = din("vc", [2, 2048, 1024])
    lfc = din("lfc", [2, 2048, 8])
    scm = din("scm", [2, 2, 512])
    scf_T = din("scf_T", [2, DFF, 2])
    ident_d = din("ident", [128, 128])
    ut_d = din("ut", [128, 128])
    e127_d = din("e127", [128, 128])
    negmask_d = din("negmask", [128, 8, 128])
    foxband_d = din("foxband", [128, 8, 128])
    foxnew_d = din("foxnew", [32, 32])
    selA_d = din("selA", [128, 8])
    selB_d = din("selB", [128, 1])
    oh_d = din("oh", [128, 8])
    y_o = dout("y", [NTOK, D])
    o_idx = dout("o_idx", [2, NTOK, 64])
    o_dk = dout("o_dk", [2, NTOK, 512])
    o_dv = dout("o_dv", [2, NTOK, 512])
    o_fk = dout("o_fk", [2, NTOK, 512])
    o_fv = dout("o_fv", [2, NTOK, 512])
    o_lf = dout("o_lf", [2, NTOK, 8])
    o_cm = dout("o_cm", [2, 2, 2, 512])
    o_cf = dout("o_cf", [2, 2, DFF, 2])
    Z = dscr("Z", [NTOK, NIN])
    XS = dscr("XS", [NTOK, D])
    MODD = dscr("MODD", [2, 2, 6, D])
    QT = dscr("QT", [12, 128, NTOK], BF16)
    WI = dscr("WI", [NTOK, 8])
    KT_in = dscr("KT_in", [1088, 2048], BF16)
    KT_all = dscr("KT_all", [8 * 1088, 2048], BF16)
    V_in = dscr("V_in", [1024, 16 * 130], BF16)
    V_all = dscr("V_all", [8 * 1024, 16 * 130], BF16)
    LF_in = dscr("LF_in", [2048, 8])
    LF_all = dscr("LF_all", [16384, 8])
    LFS_new = dscr("LFS_new", [32, 8])
    UH_in = dscr("UH_in", [16, 1024])
    UH_all = dscr("UH_all", [128, 1024])
    KT_s = dscr("KT_s", [1088, NTOK], BF16)
    V_s = dscr("V_s", [1024, 17 * 130], BF16)
    U = dscr("U", [NRG, 130, 512])
    YA = dscr("YA", [NTOK, D])
    UG = dscr("UG", [DFF, NRG, 130])
    UGH_in = dscr("UGH_in", [DFF, 32])
    UGH_all = dscr("UGH_all", [8 * DFF, 32])
    AT = dscr("AT", [22, 128, NTOK], BF16)

    RG8 = [list(range(8))]

    with ExitStack() as top:
        k = KB(nc, top)
        pe, act, dve, pool, sp = k.pe, k.act, k.dve, k.pool, k.sp
        tcount = [0]

        def T(stack, shape, dt=F32):
            tcount[0] += 1
            return TL(stack.enter_context(nc.sbuf_tensor("t%d" % tcount[0], list(shape), dt)))

        ps = []
        for i in range(8):
            ps.append(TL(top.enter_context(nc.psum_tensor("ps%d" % i, [128, 512], F32))))

        ident = T(top, [128, 128])
        k.dma(sp, ident[:], ident_d[:, :], writes=[ident])
        selA = T(top, [128, 8]); selB = T(top, [128, 1]); oh = T(top, [128, 8])
        k.dma(sp, selA[:], selA_d[:, :], writes=[selA])
        k.dma(sp, selB[:], selB_d[:, :], writes=[selB])
        k.dma(sp, oh[:], oh_d[:, :], writes=[oh])
        ohn = T(top, [128, 8])
        k.op(dve, lambda: nc.vector.tensor_scalar(out=ohn[:], in0=oh[:], scalar1=-1.0, scalar2=None, op0=ALU.mult),
             [oh], [ohn])
        cneg_p = T(top, [128, 1024]); excl_p = T(top, [128, 1024]); cqm_p = T(top, [128, 16, 8])
        cneg_s = T(top, [128, 136]); excl_s = T(top, [128, 136]); cqm_s = T(top, [128, 8])

        evc = [0]

        def evac(out, in_, reads, writes, scale=None):
            evc[0] += 1
            if evc[0] % 2 == 0:
                if scale is None:
                    k.op(act, lambda: nc.scalar.copy(out=out, in_=in_), reads, writes)
                else:
                    k.op(act, lambda: nc.scalar.mul(out=out, in_=in_, mul=float(scale)), reads, writes)
            else:
                if scale is None:
                    k.op(dve, lambda: nc.vector.tensor_copy(out=out, in_=in_), reads, writes)
                else:
                    k.op(dve, lambda: nc.vector.tensor_scalar(out=out, in0=in_, scalar1=float(scale), scalar2=None,
                                                              op0=ALU.mult), reads, writes)

        def mm(out, lhsT, rhs, start, stop, reads, writes, sig=True, **kw):
            k.op(pe, lambda: nc.tensor.matmul(out, lhsT=lhsT, rhs=rhs, start=start, stop=stop, **kw), reads, writes,
                 sig=sig)

        def transposes(src, src_c0, n, rows, dst, dst_k0, dst_c0, bank_ids, scale=None):
            q = 0
            bi = 0
            while q < n:
                g = min(4, n - q)
                bank = ps[bank_ids[bi % len(bank_ids)]]
                bi += 1
                for a in range(g):
                    c0 = src_c0 + (q + a) * 128
                    k.op(pe, lambda a=a, c0=c0: nc.tensor.transpose(
                        out=bank[:, a * 128:a * 128 + rows], in_=src[:rows, c0:c0 + 128],
                        identity=ident[:rows, :rows]), [src, ident], [bank], sig=(a == g - 1))
                evac(dst[:, dst_k0 + q:dst_k0 + q + g, dst_c0:dst_c0 + rows],
                     bank[:, 0:g * 128].rearrange("p (a b) -> p a b", b=128)[:, :, 0:rows],
                     [bank], [dst], scale)
                q += g

        def rms_rstd(stack_tiles, src, rows, srcs_res, from_psum2=None):
            junk, ss, ss2, rs = stack_tiles
            if from_psum2 is None:
                k.op(act, lambda: nc.scalar.activation(out=junk[:rows, :], in_=src[:rows, :], func=AF.Square,
                                                       accum_out=ss[:rows, :]), srcs_res, [junk, ss])
            else:
                b0, b1 = from_psum2
                k.op(act, lambda: nc.scalar.activation(out=junk[:rows, 0:512], in_=b0[:rows, :], func=AF.Square,
                                                       accum_out=ss[:rows, :]), [b0], [junk, ss])
                k.op(act, lambda: nc.scalar.activation(out=junk[:rows, 512:1024], in_=b1[:rows, :], func=AF.Square,
                                                       accum_out=ss2[:rows, :]), [b1], [junk, ss2])
                k.op(dve, lambda: nc.vector.tensor_tensor(out=ss[:rows, :], in0=ss[:rows, :], in1=ss2[:rows, :],
                                                          op=ALU.add), [ss, ss2], [ss])
            k.op(dve, lambda: nc.vector.tensor_scalar(out=ss[:rows, :], in0=ss[:rows, :], scalar1=1.0 / D,
                                                      scalar2=EPS, op0=ALU.mult, op1=ALU.add), [ss], [ss])
            k.op(act, lambda: nc.scalar.activation(out=ss[:rows, :], in_=ss[:rows, :], func=AF.Sqrt), [ss], [ss])
            k.op(dve, lambda: nc.vector.reciprocal(out=rs[:rows, :], in_=ss[:rows, :]), [ss], [rs])
            return rs

        def load_w_bf16(stack, src_ap, kc, n, wf_tiles, tag):
            wb = T(stack, [128, kc, n], BF16)
            for c in range(kc):
                wf = wf_tiles[c % len(wf_tiles)]
                k.dma(sp, wf[:, 0:n], src_ap[c * 128:(c + 1) * 128, :], writes=[wf])
                evac(wb[:, c, :], wf[:, 0:n], [wf], [wb])
            return wb

        with ExitStack() as ph:
            cT = T(ph, [128, 8, 2]); sT = T(ph, [128, 8, 2], BF16)
            k.dma(sp, cT[:], cvecT[:, :, :], writes=[cT])
            k.op(act, lambda: nc.scalar.activation(out=sT[:], in_=cT[:], func=AF.Silu), [cT], [sT])
            wst = [T(ph, [128, 8, 512]) for _ in range(2)]
            wbf = [T(ph, [128, 8, 512], BF16) for _ in range(2)]
            mod = T(ph, [2, 6 * D]); bada = T(ph, [2, 6 * D]); ng = T(ph, [2, 4 * D]); modv = T(ph, [2, 6, D])
            for l in range(2):
                k.dma(sp, bada[:], b_ada[l].partition_broadcast(2), writes=[bada])
                k.dma(sp, ng[:], norm_g[l].partition_broadcast(2), writes=[ng])
                for nb in range(12):
                    wf = wst[nb % 2]; wb = wbf[nb % 2]
                    k.dma(sp, wf[:], w_ada[l][:, nb * 512:(nb + 1) * 512].rearrange("(kc p) n -> p kc n", p=128),
                          writes=[wf])
                    evac(wb[:, 0:4, :], wf[:, 0:4, :], [wf], [wb])
                    evac(wb[:, 4:8, :], wf[:, 4:8, :], [wf], [wb])
                    for kc in range(8):
                        mm(ps[0][0:2, :], sT[:, kc, :], wb[:, kc, :], kc == 0, kc == 7, [sT, wb], [ps[0]], sig=(kc == 7))
                    k.op(dve, lambda nb=nb: nc.vector.tensor_tensor(
                        out=mod[:, nb * 512:(nb + 1) * 512], in0=ps[0][0:2, :], in1=bada[:, nb * 512:(nb + 1) * 512],
                        op=ALU.add), [ps[0], bada], [mod])
                k.op(dve, lambda: nc.vector.scalar_tensor_tensor(out=modv[:, 0, :], in0=mod[:, D:2 * D], scalar=1.0,
                                                                 in1=ng[:, 0:D], op0=ALU.add, op1=ALU.mult),
                     [mod, ng], [modv])
                k.op(dve, lambda: nc.vector.tensor_copy(out=modv[:, 1, :], in_=mod[:, 0:D]), [mod], [modv])
                k.op(dve, lambda: nc.vector.tensor_tensor(out=modv[:, 2, :], in0=mod[:, 2 * D:3 * D],
                                                          in1=ng[:, D:2 * D], op=ALU.mult), [mod, ng], [modv])
                k.op(dve, lambda: nc.vector.scalar_tensor_tensor(out=modv[:, 3, :], in0=mod[:, 4 * D:5 * D], scalar=1.0,
                                                                 in1=ng[:, 2 * D:3 * D], op0=ALU.add, op1=ALU.mult),
                     [mod, ng], [modv])
                k.op(dve, lambda: nc.vector.tensor_copy(out=modv[:, 4, :], in_=mod[:, 3 * D:4 * D]), [mod], [modv])
                k.op(dve, lambda: nc.vector.tensor_tensor(out=modv[:, 5, :], in0=mod[:, 5 * D:6 * D],
                                                          in1=ng[:, 3 * D:4 * D], op=ALU.mult), [mod, ng], [modv])
                k.dma(pool, MODD[l].rearrange("g s d -> g (s d)"), modv[:].rearrange("g s d -> g (s d)"),
                      reads=[modv])
            k.barrier()

        def load_mod(stack, l, which):
            out = []
            for g in range(2):
                t = T(stack, [128, D])
                k.dma(sp, t[:], MODD[l, g, which, :].partition_broadcast(128), writes=[t])
                out.append(t)
            return out

        import os as _os
        for l in range(int(_os.environ.get("KNL", "2"))):
            xsrc = x_in if l == 0 else XS
            with ExitStack() as ph:
                hT = T(ph, [128, 8, NTOK], BF16)
                A1b = load_mod(ph, l, 0); B1b = load_mod(ph, l, 1)
                xts = [T(ph, [128, D]) for _ in range(2)]
                hts = [T(ph, [128, D]) for _ in range(2)]
                st = (T(ph, [128, D]), T(ph, [128, 1]), T(ph, [128, 1]), T(ph, [128, 1]))
                for i in range(NRG):
                    rows, ro, g = rg_info(i)
                    xt = xts[i % 2]; ht = hts[i % 2]
                    k.dma(sp, xt[:rows, :], xsrc[ro:ro + rows, :], writes=[xt])
                    rs = rms_rstd(st, xt, rows, [xt])
                    k.op(dve, lambda: nc.vector.scalar_tensor_tensor(
                        out=ht[:rows, :], in0=xt[:rows, :], scalar=rs[:rows, 0:1], in1=A1b[g][:rows, :],
                        op0=ALU.mult, op1=ALU.mult), [xt, rs, A1b[g]], [ht])
                    k.op(dve, lambda: nc.vector.tensor_tensor(out=ht[:rows, :], in0=ht[:rows, :], in1=B1b[g][:rows, :],
                                                              op=ALU.add), [ht, B1b[g]], [ht])
                    transposes(ht, 0, 8, rows, hT, 0, ro, [0, 1, 2, 3])
                wst = [T(ph, [128, 8, 512]) for _ in range(2)]
                wbf = [T(ph, [128, 8, 512], BF16) for _ in range(2)]
                zts = [T(ph, [128, 512]) for _ in range(4)]
                n = 0
                for cb in range(17):
                    c0 = cb * 512
                    bw = min(512, NIN - c0)
                    wf = wst[cb % 2]; wb = wbf[cb % 2]
                    k.dma(sp, wf[:, :, 0:bw], w_in[l][:, c0:c0 + bw].rearrange("(kc p) n -> p kc n", p=128),
                          writes=[wf])
                    evac(wb[:, 0:4, 0:bw], wf[:, 0:4, 0:bw], [wf], [wb])
                    evac(wb[:, 4:8, 0:bw], wf[:, 4:8, 0:bw], [wf], [wb])
                    for i in range(NRG):
                        rows, ro, g = rg_info(i)
                        bank = ps[4 + n % 4]; zt = zts[n % 4]; n += 1
                        for kc in range(8):
                            mm(bank[:rows, 0:bw], hT[:, kc, ro:ro + rows], wb[:, kc, 0:bw], kc == 0, kc == 7,
                               [hT, wb], [bank], sig=(kc == 7))
                        evac(zt[:rows, 0:bw], bank[:rows, 0:bw], [bank], [zt])
                        k.dma(pool, Z[ro:ro + rows, c0:c0 + bw], zt[:rows, 0:bw], reads=[zt])
                k.barrier()

            with ExitStack() as ph:
                zts = [T(ph, [128, 3664]) for _ in range(2)]
                zcs = [T(ph, [128, 1024]) for _ in range(2)]
                kqs = [T(ph, [128, 21, 128], BF16) for _ in range(2)]
                vts = [T(ph, [128, 16, 65], BF16) for _ in range(2)]
                uts = [T(ph, [128, 512]) for _ in range(2)]
                lts = [T(ph, [128, 8]) for _ in range(2)]
                bfb = T(ph, [128, 8])
                k.dma(sp, bfb[:], b_forget[l].partition_broadcast(128), writes=[bfb])
                for vt in vts:
                    k.op(dve, lambda vt=vt: nc.vector.memset(vt[:, :, 64:65], 1.0), [], [vt])
                k.dma(pool, U[16, 0:2, :], scm[l], reads=[])
                for i in range(NRG):
                    rows, ro, g = rg_info(i)
                    zt = zts[i % 2]; zc = zcs[i % 2]; kq = kqs[i % 2]; vt = vts[i % 2]; ut = uts[i % 2]; lt = lts[i % 2]
                    k.dma(sp, zt[:rows, :], Z[ro:ro + rows, 0:3664], writes=[zt])
                    k.dma(sp, zc[:rows, :], Z[ro:ro + rows, C_CC:C_CC + 1024], writes=[zc])
                    for (o, c0, w) in ((o_dk, C_KA, 512), (o_dv, C_VA, 512), (o_idx, C_KI, 64), (o_fk, C_KB, 512),
                                       (o_fv, C_VB, 512)):
                        k.dma(pool, o[l, ro:ro + rows, :], zt[:rows, c0:c0 + w], reads=[zt])
                    k.dma(pool, WI[ro:ro + rows, :], zt[:rows, C_WI:C_WI + 8], reads=[zt])
                    k.op(dve, lambda: nc.vector.tensor_tensor(out=lt[:rows, :], in0=zt[:rows, C_FL:C_FL + 8],
                                                              in1=bfb[:rows, :], op=ALU.add), [zt, bfb], [lt])
                    k.op(act, lambda: nc.scalar.activation(out=lt[:rows, :], in_=lt[:rows, :], func=AF.Exp, scale=-1.0),
                         [lt], [lt])
                    k.op(act, lambda: nc.scalar.activation(out=lt[:rows, :], in_=lt[:rows, :], func=AF.Ln, bias=1.0),
                         [lt], [lt])
                    k.op(dve, lambda: nc.vector.tensor_scalar(out=lt[:rows, :], in0=lt[:rows, :], scalar1=-1.0,
                                                              scalar2=None, op0=ALU.mult), [lt], [lt])
                    k.dma(pool, o_lf[l, ro:ro + rows, :], lt[:rows, :], reads=[lt])
                    if i < 16:
                        k.dma(pool, LF_in[ro:ro + rows, :], lt[:rows, :], reads=[lt])
                    else:
                        k.dma(pool, LFS_new[:, :], lt[:rows, :], reads=[lt])
                    for si, (c0, sc) in enumerate(((C_QA, 0.125), (C_KA, None), (C_QI, None), (C_QB, 0.125),
                                                   (C_KB, None))):
                        transposes(zt, c0, 4, rows, kq, si * 4, 0, [si % 4], sc)
                    k.op(pe, lambda: nc.tensor.transpose(out=ps[5][0:64, 0:rows], in_=zt[:rows, C_KI:C_KI + 64],
                                                         identity=ident[:rows, :rows]), [zt, ident], [ps[5]])
                    evac(kq[0:64, 20, 0:rows], ps[5][0:64, 0:rows], [ps[5]], [kq])
                    for (qa, ka) in ((0, 0), (4, 8), (8, 12)):
                        k.dma(pool, QT[qa:qa + 4, :, ro:ro + rows].rearrange("k p t -> p k t"),
                              kq[:, ka:ka + 4, 0:rows], reads=[kq])
                    if i < 16:
                        KTd = KT_in; co = ro
                    else:
                        KTd = KT_s; co = 2048
                    k.dma(pool, KTd[0:512, co:co + rows].rearrange("(k p) t -> p k t", p=128), kq[:, 4:8, 0:rows],
                          reads=[kq])
                    k.dma(pool, KTd[512:1024, co:co + rows].rearrange("(k p) t -> p k t", p=128), kq[:, 16:20, 0:rows],
                          reads=[kq])
                    k.dma(pool, KTd[1024:1088, co:co + rows], kq[0:64, 20, 0:rows], reads=[kq])
                    evac(vt[:rows, 0:8, 0:64], zt[:rows, C_VA:C_VA + 512].rearrange("p (h d) -> p h d", d=64), [zt], [vt])
                    evac(vt[:rows, 8:16, 0:64], zt[:rows, C_VB:C_VB + 512].rearrange("p (h d) -> p h d", d=64), [zt], [vt])
                    if i < 16:
                        vd = V_in.rearrange("(q p) (j c) -> p q j c", p=128, c=130)[:, :, i, :]
                    else:
                        vd = V_s.rearrange("(q p) (j c) -> p q j c", p=128, c=130)[0:32, :, 16, :]
                    k.dma(pool, vd, vt[:rows, :, :].rearrange("p (q e) c -> p q (e c)", e=2), reads=[vt])
                    k.op(dve, lambda: nc.vector.tensor_tensor(out=ut[:rows, :], in0=zc[:rows, 0:512],
                                                              in1=zc[:rows, 512:1024], op=ALU.mult), [zc], [ut])
                    k.dma(pool, U[i, 2:2 + rows, :], ut[:rows, :], reads=[ut])
                    if i < 16:
                        k.dma(pool, UH_in[i:i + 1, :].rearrange("a (t c) -> (a t) c", t=2), ut[126:128, :], reads=[ut])
                    if i == 15:
                        k.dma(pool, o_cm[l, 0], ut[126:128, :], reads=[ut])
                    if i == 16:
                        k.dma(pool, o_cm[l, 1], ut[30:32, :], reads=[ut])
                k.barrier()

            for (src, dst) in ((KT_in, KT_all), (V_in, V_all), (LF_in, LF_all), (UH_in, UH_all)):
                k.op(pool, lambda src=src, dst=dst: nc.gpsimd.collective_compute(
                    "AllGather", ALU.bypass, replica_groups=RG8, ins=[src[:, :]], outs=[dst[:, :]]), [], [])
            k.barrier()
            with ExitStack() as ph:
                hA = T(ph, [16, 8, 1024]); hB = T(ph, [16, 1024]); hacc = T(ph, [16, 1024])
                k.dma(sp, hA[:], UH_all.rearrange("(r j) n -> j r n", r=8), writes=[hA])
                k.op(dve, lambda: nc.vector.memset(hB[:], 0.0), [], [hB])
                k.dma(sp, hB[1:16, :], UH_all[7 * 16:7 * 16 + 15, :], writes=[hB])
                k.op(dve, lambda: nc.vector.tensor_scalar(out=hacc[:], in0=hB[:], scalar1=selB[0:16, 0:1], scalar2=None,
                                                          op0=ALU.mult), [hB, selB], [hacc])
                for r in range(8):
                    k.op(dve, lambda r=r: nc.vector.scalar_tensor_tensor(
                        out=hacc[:], in0=hA[:, r, :], scalar=selA[0:16, r:r + 1], in1=hacc[:], op0=ALU.mult,
                        op1=ALU.add), [hA, selA, hacc], [hacc])
                k.dma(pool, U[0:16, 0:2, :], hacc[:].rearrange("j (t c) -> j t c", t=2), reads=[hacc])
                kfs = [T(ph, [128, 2048]) for _ in range(2)]
                kbs = [T(ph, [128, 2048], BF16) for _ in range(2)]
                for m in range(9):
                    m0 = m * 128
                    mr = min(128, 1088 - m0)
                    kf = kfs[m % 2]; kb_ = kbs[m % 2]
                    k.dma(sp, kf[:mr, :], kc_T[l, m0:m0 + mr, :], writes=[kf])
                    evac(kb_[:mr, :], kf[:mr, :], [kf], [kb_])
                    k.dma(pool, KT_s[m0:m0 + mr, 0:2048], kb_[:mr, :], reads=[kb_])
                vfs = [T(ph, [128, 1024]) for _ in range(2)]
                vts = [T(ph, [128, 16, 65], BF16) for _ in range(2)]
                for vt in vts:
                    k.op(dve, lambda vt=vt: nc.vector.memset(vt[:, :, 64:65], 1.0), [], [vt])
                for t in range(16):
                    vf = vfs[t % 2]; vt = vts[t % 2]
                    k.dma(sp, vf[:], vc[l, t * 128:(t + 1) * 128, :], writes=[vf])
                    evac(vt[:, :, 0:64], vf[:].rearrange("p (h d) -> p h d", d=64), [vf], [vt])
                    k.dma(pool, V_s.rearrange("(q p) (j c) -> p q j c", p=128, c=130)[:, :, t, :],
                          vt[:].rearrange("p (q e) c -> p q (e c)", e=2), reads=[vt])
                utm = T(ph, [128, 128]); e127 = T(ph, [128, 128]); ones = T(ph, [128, 128])
                k.dma(sp, utm[:], ut_d[:, :], writes=[utm])
                k.dma(sp, e127[:], e127_d[:, :], writes=[e127])
                k.op(dve, lambda: nc.vector.memset(ones[:], 1.0), [], [ones])
                lfp = T(ph, [128, 1024]); lfs = T(ph, [128, 136])
                csb = T(ph, [128, 1024]); tb = T(ph, [128, 1024])
                for r in range(8):
                    k.dma(sp, lfp[:].rearrange("p (j r h) -> p j r h", j=16, r=8)[:, :, r, :],
                          LF_all.rearrange("(r j p) h -> p j r h", r=8, j=16)[:, :, r, :], writes=[lfp])
                k.op(dve, lambda: nc.vector.memset(lfs[:], 0.0), [], [lfs])
                k.dma(sp, lfs[:, 0:128].rearrange("p (t h) -> p t h", h=8), lfc[l].rearrange("(t p) h -> p t h", p=128),
                      writes=[lfs])
                k.dma(sp, lfs[0:32, 128:136], LFS_new[:, :], writes=[lfs])

                incl = T(ph, [128, 1024])

                def cum_tables(lf, n, cneg, excl):
                    nt = n // 8
                    for c0 in range(0, n, 512):
                        cw = min(512, n - c0)
                        b = ps[0]
                        mm(b[:, 0:cw], utm[:], lf[:, c0:c0 + cw], True, True, [utm, lf], [b])
                        k.op(dve, lambda: nc.vector.tensor_copy(out=csb[:, c0:c0 + cw], in_=b[:, 0:cw]), [b], [csb])
                    for c0 in range(0, n, 512):
                        cw = min(512, n - c0)
                        b = ps[1]
                        mm(b[:, 0:cw], e127[:], csb[:, c0:c0 + cw], True, True, [e127, csb], [b])
                        k.op(dve, lambda: nc.vector.tensor_copy(out=tb[:, c0:c0 + cw], in_=b[:, 0:cw]), [b], [tb])
                    for h in range(8):
                        k.op(dve, lambda h=h: nc.vector.tensor_tensor_scan(
                            out=incl[:, h:n:8], data0=ones[:, 0:nt], data1=tb[:, h:n:8], initial=0.0,
                            op0=ALU.mult, op1=ALU.add), [ones, tb], [incl])
                    k.op(dve, lambda: nc.vector.tensor_tensor(out=excl[:, 0:n], in0=incl[:, 0:n], in1=tb[:, 0:n],
                                                              op=ALU.subtract), [tb, incl], [excl])
                    k.op(dve, lambda: nc.vector.scalar_tensor_tensor(
                        out=cneg[:, 0:n], in0=csb[:, 0:n], scalar=-1.0, in1=excl[:, 0:n], op0=ALU.mult,
                        op1=ALU.subtract), [excl, csb], [cneg])

                cum_tables(lfp, 1024, cneg_p, excl_p)
                cum_tables(lfs, 136, cneg_s, excl_s)
                cnv = cneg_p[:].rearrange("p (j r h) -> p j r h", j=16, r=8)
                for r in range(8):
                    if r == 0:
                        k.op(dve, lambda: nc.vector.tensor_scalar(out=cqm_p[:], in0=cnv[:, :, 0, :], scalar1=oh[:, 0:1],
                                                                  scalar2=-1.0, op0=ALU.mult, op1=ALU.mult),
                             [cneg_p, oh], [cqm_p])
                    else:
                        k.op(dve, lambda r=r: nc.vector.scalar_tensor_tensor(
                            out=cqm_p[:], in0=cnv[:, :, r, :], scalar=ohn[:, r:r + 1], in1=cqm_p[:], op0=ALU.mult,
                            op1=ALU.add), [cneg_p, ohn, cqm_p], [cqm_p])
                k.op(dve, lambda: nc.vector.tensor_scalar(out=cqm_s[:], in0=cneg_s[:, 128:136], scalar1=-1.0,
                                                          scalar2=None, op0=ALU.mult), [cneg_s], [cqm_s])
                k.barrier()

            with ExitStack() as ph:
                scores = T(ph, [128, 16384])
                selT = T(ph, [128, 128, 128], BF16)
                kiT = T(ph, [128, 8, 1024], BF16)
                kiTs = T(ph, [64, NTOK], BF16)
                for hf in range(2):
                    k.dma(sp, kiT[hf * 64:(hf + 1) * 64, :, :],
                          KT_all.rearrange("(r m) t -> m r t", r=8)[1024:1088, :, hf * 1024:(hf + 1) * 1024],
                          writes=[kiT])
                k.dma(sp, kiTs[:, :], KT_s[1024:1088, :], writes=[kiTs])
                negmask = T(ph, [128, 8, 128])
                k.dma(sp, negmask[:], negmask_d[:, :, :], writes=[negmask])
                fbadd = T(ph, [128, 8, 128]); fnadd = T(ph, [32, 32])
                k.dma(sp, fbadd[:], foxband_d[:, :, :], writes=[fbadd])
                k.dma(sp, fnadd[:], foxnew_d[:, :], writes=[fnadd])
                qiT = T(ph, [128, 8, 128], BF16)
                wi = T(ph, [128, 8]); absw = T(ph, [128, 8]); sgn = T(ph, [128, 8])
                rts = [T(ph, [128, 512]) for _ in range(4)]
                bs = [T(ph, [128, 1]) for _ in range(6)]
                kTb = [T(ph, [128, 2048], BF16) for _ in range(2)]
                Vb = [T(ph, [128, 16, 130], BF16) for _ in range(2)]
                Vp = [T(ph, [128, 16, 130], BF16) for _ in range(2)]
                PT = [T(ph, [128, 2, 2, 128], BF16) for _ in range(6)]
                PV_SKEW = 2
                qts = [T(ph, [128, 2, 128], BF16) for _ in range(2)]
                for qt_ in qts:
                    k.op(dve, lambda qt_=qt_: nc.vector.memset(qt_[:], 0.0), [], [qt_])
                yt = T(ph, [128, 512]); rd = T(ph, [128, 2])
                wg = T(ph, [128, 1024]); rho = T(ph, [128, 8]); dgB = T(ph, [128, 128]); onesf = T(ph, [128, 128])
                ocomb = T(ph, [128, 130])
                dargs = [T(ph, [128, 128]) for _ in range(2)]
                dms = [T(ph, [128, 128], BF16) for _ in range(2)]
                k.op(dve, lambda: nc.vector.memset(onesf[:], 1.0), [], [onesf])
                cnt_ = {"s": 0, "kv": 0, "q": 0}

                for qg in range(NRG):
                    qw, q0, g = rg_info(qg)
                    if qg < 16:
                        Lr = (qg + 1) * 128
                        blocks = [dict(nt=qg + 1, ks=128, r=r, kcol=(r, 0), tb=r * (qg + 1)) for r in range(8)]
                        L = 8 * Lr
                    else:
                        blocks = [dict(nt=16, ks=128, r=None, kcol=0, tb=0), dict(nt=1, ks=32, r=None, kcol=2048, tb=16)]
                        L = NTOK
                    NT = (L + 127) // 128
                    for hf in range(2):
                        for e in range(2):
                            k.dma(sp, qiT[hf * 64:(hf + 1) * 64, e:8:2, 0:qw],
                                  QT[4:8, e * 64:(e + 1) * 64, q0:q0 + qw].rearrange("k p t -> p k t"), writes=[qiT])
                    k.dma(sp, wi[:qw, :], WI[q0:q0 + qw, :], writes=[wi])
                    k.op(dve, lambda: nc.vector.tensor_scalar(out=sgn[:qw, :], in0=wi[:qw, :], scalar1=0.0, scalar2=2.0,
                                                              op0=ALU.is_ge, op1=ALU.mult), [wi], [sgn])
                    k.op(dve, lambda: nc.vector.tensor_scalar(out=sgn[:qw, :], in0=sgn[:qw, :], scalar1=-1.0,
                                                              scalar2=None, op0=ALU.add), [sgn], [sgn])
                    k.op(dve, lambda: nc.vector.tensor_tensor(out=absw[:qw, :], in0=wi[:qw, :], in1=sgn[:qw, :],
                                                              op=ALU.mult), [wi, sgn], [absw])
                    soff = 0
                    for blk in blocks:
                        nk = (blk["nt"] - 1) * 128 + blk["ks"]
                        for c0 in range(0, nk, 512):
                            cw = min(512, nk - c0)
                            for h in range(8):
                                bank = ps[h % 4]
                                if qg < 16:
                                    hf = c0 // 1024
                                    cc = c0 % 1024
                                    rhs = kiT[hf * 64:(hf + 1) * 64, blk["r"], cc:cc + cw]
                                    krs = kiT
                                else:
                                    hf = 0
                                    rhs = kiTs[0:64, blk["kcol"] + c0:blk["kcol"] + c0 + cw]
                                    krs = kiTs
                                mm(bank[:qw, 0:cw], qiT[hf * 64:(hf + 1) * 64, h, 0:qw], rhs, True, True,
                                   [qiT, krs], [bank])
                                rt = rts[h % 4]
                                k.op(act, lambda bank=bank, rt=rt, h=h, cw=cw: nc.scalar.activation(
                                    out=rt[:qw, 0:cw], in_=bank[:qw, 0:cw], func=AF.Relu, scale=absw[:qw, h:h + 1]),
                                    [bank, absw], [rt])
                                sc = scores[:qw, soff + c0:soff + c0 + cw]
                                if h == 0:
                                    k.op(dve, lambda rt=rt, sc=sc, cw=cw: nc.vector.tensor_scalar(
                                        out=sc, in0=rt[:qw, 0:cw], scalar1=sgn[:qw, 0:1], scalar2=None, op0=ALU.mult),
                                        [rt, sgn], [scores])
                                else:
                                    k.op(dve, lambda rt=rt, sc=sc, cw=cw, h=h: nc.vector.scalar_tensor_tensor(
                                        out=sc, in0=rt[:qw, 0:cw], scalar=sgn[:qw, h:h + 1], in1=sc, op0=ALU.mult,
                                        op1=ALU.add), [rt, sgn, scores], [scores])
                        soff += nk
                    rmin, wdt, mid, cnt, tmp, rmax = bs
                    sc = scores[:qw, 0:L]

                    def bis_gen():
                        rmin, wdt, mid, cnt, tmp, rmax = bs
                        sc = scores[:qw, 0:L]
                        k.op(dve, lambda: nc.vector.tensor_reduce(out=rmin[:qw, :], in_=sc, axis=AX.X, op=ALU.min),
                             [scores], [rmin])
                        if qg < 16:
                            sv = scores[:qw, 0:L].rearrange("p (r t) -> p r t", r=8)[:, :, qg * 128:(qg + 1) * 128]
                            k.op(dve, lambda: nc.vector.tensor_tensor(out=sv, in0=sv, in1=negmask[:qw, :, :], op=ALU.add),
                                 [scores, negmask], [scores])
                        k.op(dve, lambda: nc.vector.tensor_reduce(out=rmax[:qw, :], in_=sc, axis=AX.X, op=ALU.max),
                             [scores], [rmax])
                        k.op(dve, lambda: nc.vector.tensor_tensor(out=wdt[:qw, :], in0=rmax[:qw, :], in1=rmin[:qw, :],
                                                                  op=ALU.subtract), [rmax, rmin], [wdt])
                        yield
                        junk = selT[:qw, :, :].rearrange("p a b -> p (a b)")[:, 0:L]
                        for it in range(NB_BISECT):
                            k.op(dve, lambda: nc.vector.tensor_scalar(out=wdt[:qw, :], in0=wdt[:qw, :], scalar1=0.5,
                                                                      scalar2=None, op0=ALU.mult), [wdt], [wdt])
                            k.op(dve, lambda: nc.vector.tensor_tensor(out=mid[:qw, :], in0=rmin[:qw, :], in1=wdt[:qw, :],
                                                                      op=ALU.add), [rmin, wdt], [mid])
                            k.op(dve, lambda: nc.vector.tensor_scalar(out=junk, in0=sc, scalar1=mid[:qw, 0:1], scalar2=None,
                                                                      op0=ALU.is_ge, op1=ALU.add, accum_out=cnt[:qw, :]),
                                 [scores, mid], [selT, cnt])
                            k.op(dve, lambda: nc.vector.tensor_scalar(out=tmp[:qw, :], in0=cnt[:qw, :], scalar1=255.5,
                                                                      scalar2=wdt[:qw, 0:1], op0=ALU.is_ge, op1=ALU.mult),
                                 [cnt, wdt], [tmp])
                            k.op(dve, lambda: nc.vector.tensor_tensor(out=rmin[:qw, :], in0=rmin[:qw, :], in1=tmp[:qw, :],
                                                                      op=ALU.add), [rmin, tmp], [rmin])
                            yield
                    def sel_transposes():
                        k.op(dve, lambda: nc.vector.tensor_scalar(out=sc, in0=sc, scalar1=rmin[:qw, 0:1], scalar2=None,
                                                                  op0=ALU.is_ge), [scores, rmin], [scores])
                        kt = 0
                        bi = 0
                        while kt < NT:
                            gnum = min(4, NT - kt)
                            bank = ps[4 + bi % 2]
                            bi += 1
                            kmax = 0
                            for a in range(gnum):
                                c0 = (kt + a) * 128
                                cwid = min(128, L - c0)
                                kmax = max(kmax, cwid)
                                k.op(pe, lambda a=a, c0=c0, cwid=cwid, bank=bank: nc.tensor.transpose(
                                    out=bank[0:cwid, a * 128:a * 128 + qw], in_=scores[:qw, c0:c0 + cwid],
                                    identity=ident[:qw, :qw]), [scores, ident], [bank], sig=(a == gnum - 1))
                            if kmax == 128 or gnum == 1:
                                evac(selT[0:kmax, kt:kt + gnum, 0:qw],
                                     bank[0:kmax, 0:gnum * 128].rearrange("p (a b) -> p a b", b=128)[:, :, 0:qw],
                                     [bank], [selT])
                            else:
                                raise AssertionError("unexpected partial tile in group")
                            kt += gnum
                    for blk in blocks:
                        if qg < 16:
                            blk["nprev"], blk["nband"] = blk["nt"] - 1, 1
                        elif blk["ks"] == 128:
                            blk["nprev"], blk["nband"] = 16, 0
                        else:
                            blk["nprev"], blk["nband"] = 0, 1
                    def attend(kind, hook):
                        has_prev = True
                        if kind == 1:
                            if qg < 16:
                                nW = qg * 64
                                has_prev = qg > 0
                                cst = excl_p[:].rearrange("p (j r h) -> p j r h", r=8, h=8)[:, qg:qg + 1, 0, :]
                                cqv = cqm_p[:, qg, :]
                                cn_t, cst_t, cq_t = cneg_p, excl_p, cqm_p
                            else:
                                nW = 128
                                cst = excl_s[:, 128:136].rearrange("p (a h) -> p a h", a=1)
                                cqv = cqm_s[:, :]
                                cn_t, cst_t, cq_t = cneg_s, excl_s, cqm_s
                            if has_prev:
                                wgv = wg[:, 0:nW].rearrange("p (t h) -> p t h", h=8)
                                cn = cn_t[:, 0:nW].rearrange("p (t h) -> p t h", h=8)
                                k.op(dve, lambda: nc.vector.tensor_tensor(
                                    out=wgv, in0=cn, in1=cst.to_broadcast([128, nW // 8, 8]), op=ALU.add),
                                    [cn_t, cst_t], [wg])
                                k.op(dve, lambda: nc.vector.tensor_scalar(out=wg[:, 0:nW], in0=wg[:, 0:nW], scalar1=0.0,
                                                                          scalar2=None, op0=ALU.min), [wg], [wg])
                                k.op(act, lambda: nc.scalar.activation(out=wg[:, 0:nW], in_=wg[:, 0:nW], func=AF.Exp),
                                     [wg], [wg])
                            k.op(dve, lambda: nc.vector.tensor_tensor(
                                out=rho[:, :], in0=cqv, in1=cst[:, 0, :], op=ALU.subtract), [cq_t, cst_t], [rho])
                            k.op(act, lambda: nc.scalar.activation(out=rho[:, :], in_=rho[:, :], func=AF.Exp),
                                 [rho], [rho])
                        for pr in range(4):
                            qt = qts[cnt_["q"] % 2]; cnt_["q"] += 1
                            qrow = (0 if kind == 0 else 8) + pr
                            k.dma(sp, qt[0:64, 0, 0:qw], QT[qrow, 0:64, q0:q0 + qw], writes=[qt])
                            k.dma(sp, qt[64:128, 1, 0:qw], QT[qrow, 64:128, q0:q0 + qw], writes=[qt])
                            Oa, Obd = ps[6], ps[7]
                            if kind == 1:
                                for e in range(2):
                                    h = 2 * pr + e
                                    k.op(dve, lambda h=h: nc.vector.tensor_scalar(
                                        out=dgB[:qw, 0:qw], in0=ident[:qw, 0:qw], scalar1=cqv[:qw, h:h + 1],
                                        scalar2=None, op0=ALU.mult), [ident, cq_t], [dgB])
                                    mm(ps[4 + e][0:qw, 0:qw], onesf[0:qw, 0:qw], dgB[0:qw, 0:qw], True, True,
                                       [onesf, dgB], [ps[4 + e]])
                            if kind == 0:
                                totA = sum(b["nt"] for b in blocks) * 2
                                totB = 0
                            else:
                                totA = sum(b["nprev"] for b in blocks) * 2
                                totB = sum(b["nband"] for b in blocks) * 2
                            cA = [0]; cB = [0]
                            pend = []
                            for bidx, blk in enumerate(blocks):
                                while pend and pend[0][0] <= bidx - 2:
                                    pend.pop(0)[1]()
                                nt, ks = blk["nt"], blk["ks"]
                                nk = (nt - 1) * 128 + ks
                                kbuf = kTb[cnt_["kv"] % 2]; vbuf = Vb[cnt_["kv"] % 2]; vpb = Vp[cnt_["kv"] % 2]
                                cnt_["kv"] += 1
                                krow = (0 if kind == 0 else 512) + pr * 128
                                vq = (0 if kind == 0 else 4) + pr
                                if qg < 16:
                                    r = blk["r"]
                                    ksrc = KT_all[r * 1088 + krow:r * 1088 + krow + 128, 0:nk]
                                    vsrc = V_all[r * 1024 + vq * 128:r * 1024 + (vq + 1) * 128, 0:nt * 130]
                                else:
                                    ksrc = KT_s[krow:krow + 128, blk["kcol"]:blk["kcol"] + nk]
                                    t0 = blk["tb"]
                                    vsrc = V_s[vq * 128:vq * 128 + ks, t0 * 130:(t0 + nt) * 130]
                                k.dma(sp, kbuf[:, 0:nk], ksrc, writes=[kbuf])
                                k.dma(sp, vbuf[0:ks, 0:nt, :].rearrange("p t c -> p (t c)"), vsrc, writes=[vbuf])
                                npv = blk["nprev"]
                                if kind == 1 and npv > 0:
                                    for e in range(2):
                                        h = 2 * pr + e
                                        if qg < 16:
                                            wsl = wg[:, 0:1024].rearrange("p (j r h) -> p j r h", r=8, h=8)[
                                                0:ks, 0:npv, blk["r"], h:h + 1]
                                        else:
                                            wsl = wg[:, 0:128].rearrange("p (t h) -> p t h", h=8)[0:ks, 0:npv, h:h + 1]
                                        k.op(dve, lambda e=e, wsl=wsl, vbuf=vbuf, vpb=vpb, ks=ks, npv=npv:
                                             nc.vector.tensor_tensor(
                                                 out=vpb[0:ks, 0:npv, e * 65:(e + 1) * 65],
                                                 in0=vbuf[0:ks, 0:npv, e * 65:(e + 1) * 65],
                                                 in1=wsl.to_broadcast([ks, npv, 65]), op=ALU.mult),
                                             [vbuf, wg], [vpb])
                                hook()
                                groups = []
                                nmain = nt if kind == 0 else npv
                                for g0 in range(0, nmain, 2):
                                    groups.append((list(range(g0, min(g0 + 2, nmain))), "dsa" if kind == 0 else "prev"))
                                if kind == 1 and blk["nband"]:
                                    groups.append(([nt - 1], "band"))
                                for (tiles, mode) in groups:
                                    gn = len(tiles)
                                    Sb = ps[cnt_["s"] % 4]; pt = PT[cnt_["s"] % 6]; cnt_["s"] += 1
                                    for tl, t in enumerate(tiles):
                                        mm(Sb[0:ks, tl * 256:(tl + 1) * 256].rearrange("p (e b) -> p e b", b=128)[:, :, 0:qw],
                                           kbuf[:, t * 128:t * 128 + ks], qt[:, :, 0:qw], True, True, [kbuf, qt], [Sb],
                                           sig=(tl == gn - 1))
                                    k.op(act, lambda Sb=Sb, pt=pt, gn=gn, ks=ks: nc.scalar.activation(
                                        out=pt[0:ks, 0:gn, :, 0:qw],
                                        in_=Sb[0:ks, 0:gn * 256].rearrange("p (a e b) -> p a e b", e=2, b=128)[:, :, :, 0:qw],
                                        func=AF.Exp), [Sb], [pt])
                                    if mode == "dsa":
                                        tb0 = blk["tb"] + tiles[0]
                                        for e in range(2):
                                            k.op(dve, lambda pt=pt, gn=gn, ks=ks, tb0=tb0, e=e: nc.vector.tensor_tensor(
                                                out=pt[0:ks, 0:gn, e, 0:qw], in0=pt[0:ks, 0:gn, e, 0:qw],
                                                in1=selT[0:ks, tb0:tb0 + gn, 0:qw], op=ALU.mult), [pt, selT], [pt])
                                    elif mode == "band":
                                        for e in range(2):
                                            h = 2 * pr + e
                                            if qg < 16:
                                                m0 = fbadd[:, blk["r"], :]
                                                mres = fbadd
                                                cns = cneg_p[:].rearrange("p (j r h) -> p j r h", r=8, h=8)[
                                                    0:ks, qg, blk["r"], h:h + 1]
                                            else:
                                                m0 = fnadd[:, :]
                                                mres = fnadd
                                                cns = cneg_s[0:ks, 128 + h:128 + h + 1]
                                            da = dargs[e]; dm = dms[e]
                                            k.op(dve, lambda da=da, e=e, cns=cns, m0=m0, ks=ks:
                                                 nc.vector.scalar_tensor_tensor(
                                                     out=da[0:ks, 0:qw], in0=ps[4 + e][0:ks, 0:qw], scalar=cns, in1=m0,
                                                     op0=ALU.add, op1=ALU.min), [ps[4 + e], cn_t, mres], [da])
                                            k.op(act, lambda da=da, dm=dm, ks=ks: nc.scalar.activation(
                                                out=dm[0:ks, 0:qw], in_=da[0:ks, 0:qw], func=AF.Exp), [da], [dm])
                                            k.op(dve, lambda pt=pt, dm=dm, ks=ks, e=e: nc.vector.tensor_tensor(
                                                out=pt[0:ks, 0, e, 0:qw], in0=pt[0:ks, 0, e, 0:qw], in1=dm[0:ks, 0:qw],
                                                op=ALU.mult), [pt, dm], [pt])
                                    if mode == "band":
                                        Oacc, cc_, tot, vuse = Obd, cB, totB, vbuf
                                    elif mode == "prev":
                                        Oacc, cc_, tot, vuse = Oa, cA, totA, vpb
                                    else:
                                        Oacc, cc_, tot, vuse = Oa, cA, totA, vbuf

                                    def stage2(Oacc=Oacc, cc_=cc_, tot=tot, vuse=vuse, pt=pt, tiles=tiles, ks=ks, gn=gn):
                                        for e in range(2):
                                            for tl, t in enumerate(tiles):
                                                mm(Oacc[0:qw, e * 65:(e + 1) * 65], pt[0:ks, tl, e, 0:qw],
                                                   vuse[0:ks, t, e * 65:(e + 1) * 65], cc_[0] == 0, cc_[0] == tot - 1,
                                                   [pt, vuse], [Oacc], sig=(e == 1 and tl == gn - 1),
                                                   skip_group_check=True)
                                                cc_[0] += 1
                                    pend.append((bidx, stage2))
                                    if len(pend) > PV_SKEW:
                                        pend.pop(0)[1]()
                            while pend:
                                pend.pop(0)[1]()
                            if kind == 0:
                                src = Oa
                            else:
                                k.op(act, lambda: nc.scalar.copy(out=ocomb[:qw, :], in_=Obd[:qw, 0:130]), [Obd], [ocomb])
                                if has_prev:
                                    for e in range(2):
                                        h = 2 * pr + e
                                        k.op(dve, lambda e=e, h=h: nc.vector.scalar_tensor_tensor(
                                            out=ocomb[:qw, e * 65:(e + 1) * 65], in0=Oa[:qw, e * 65:(e + 1) * 65],
                                            scalar=rho[:qw, h:h + 1], in1=ocomb[:qw, e * 65:(e + 1) * 65],
                                            op0=ALU.mult, op1=ALU.add), [Oa, rho, ocomb], [ocomb])
                                src = ocomb
                            k.op(dve, lambda src=src: nc.vector.reciprocal(out=rd[:qw, :], in_=src[:qw, 64:130:65]),
                                 [src], [rd])
                            for e in range(2):
                                hh = 2 * pr + e
                                k.op(dve, lambda src=src, e=e, hh=hh: nc.vector.tensor_scalar(
                                    out=yt[:qw, hh * 64:(hh + 1) * 64], in0=src[:qw, e * 65:e * 65 + 64],
                                    scalar1=rd[:qw, e:e + 1], scalar2=None, op0=ALU.mult), [src, rd], [yt])
                        k.dma(pool, YA[q0:q0 + qw, kind * 512:(kind + 1) * 512], yt[:qw, :], reads=[yt])

                    _g = bis_gen()
                    _per = -(-(NB_BISECT + 1) // (4 * len(blocks)))

                    def _hook():
                        for _ in range(_per):
                            next(_g, None)

                    attend(1, _hook)
                    for _ in _g:
                        pass
                    sel_transposes()
                    attend(0, lambda: None)
                k.barrier()

            with ExitStack() as phh:
              h2T = T(phh, [128, 8, NTOK], BF16)
              with ExitStack() as ph:
                wfs = [T(ph, [128, 1024]) for _ in range(2)]
                Wb = load_w_bf16(ph, w_branch[l], 12, 1024, wfs, "wb")
                Wo = load_w_bf16(ph, w_out[l], 8, 1024, wfs, "wo")
                cw = T(ph, [128, 3, 512])
                k.dma(sp, cw[:].rearrange("p a c -> p (a c)"), conv_mix_w[l].partition_broadcast(128), writes=[cw])
                G1b = load_mod(ph, l, 2); A2b = load_mod(ph, l, 3); B2b = load_mod(ph, l, 4)
                us = [T(ph, [128, 3, 512]) for _ in range(1)]
                cbs = [T(ph, [128, 512]) for _ in range(2)]
                Ys = [T(ph, [128, 1536]) for _ in range(1)]
                gls = [T(ph, [128, 3072]) for _ in range(1)]
                xts = [T(ph, [128, D]) for _ in range(2)]
                yT = T(ph, [128, 12, 128], BF16); mT = T(ph, [128, 8, 128], BF16)
                mix = T(ph, [128, D]); tmpm = T(ph, [128, 512]); hh = T(ph, [128, D])
                st = (T(ph, [128, D]), T(ph, [128, 1]), T(ph, [128, 1]), T(ph, [128, 1]))
                for i in range(NRG):
                    rows, ro, g = rg_info(i)
                    u3 = us[0]; cb_ = cbs[i % 2]; Y = Ys[0]; gl = gls[0]; xt = xts[i % 2]
                    for s in range(3):
                        k.dma(sp, u3[:rows, s, :], U[i, s:s + rows, :], writes=[u3])
                    k.dma(sp, cb_[:rows, :], Z[ro:ro + rows, C_CB:C_CB + 512], writes=[cb_])
                    k.dma(sp, Y[:rows, 0:1024], YA[ro:ro + rows, :], writes=[Y])
                    k.dma(sp, gl[:rows, :], Z[ro:ro + rows, C_GL:C_GL + 3072], writes=[gl])
                    k.dma(sp, xt[:rows, :], xsrc[ro:ro + rows, :], writes=[xt])
                    yc = Y[:rows, 1024:1536]
                    k.op(dve, lambda: nc.vector.tensor_tensor(out=yc, in0=u3[:rows, 0, :], in1=cw[:rows, 0, :],
                                                              op=ALU.mult), [u3, cw], [Y])
                    for s in (1, 2):
                        k.op(dve, lambda s=s: nc.vector.tensor_tensor(out=u3[:rows, s, :], in0=u3[:rows, s, :],
                                                                      in1=cw[:rows, s, :], op=ALU.mult), [u3, cw], [u3])
                        k.op(dve, lambda s=s: nc.vector.tensor_tensor(out=yc, in0=yc, in1=u3[:rows, s, :], op=ALU.add),
                             [Y, u3], [Y])
                    k.op(dve, lambda: nc.vector.tensor_tensor(out=yc, in0=yc, in1=cb_[:rows, :], op=ALU.mult),
                         [Y, cb_], [Y])
                    transposes(Y, 0, 12, rows, yT, 0, 0, [0, 1, 2])
                    k.op(act, lambda: nc.scalar.activation(out=gl[:rows, :], in_=gl[:rows, :], func=AF.Sigmoid),
                         [gl], [gl])
                    for nb in range(2):
                        for n in range(3):
                            bank = ps[4 + (nb * 3 + n) % 3]
                            for kc in range(4):
                                mm(bank[:rows, :], yT[:, n * 4 + kc, 0:rows], Wb[:, n * 4 + kc, nb * 512:(nb + 1) * 512],
                                   kc == 0, kc == 3, [yT, Wb], [bank], sig=(kc == 3))
                            gsl = gl[:rows, n * 1024 + nb * 512:n * 1024 + (nb + 1) * 512]
                            msl = mix[:rows, nb * 512:(nb + 1) * 512]
                            if n == 0:
                                k.op(dve, lambda bank=bank, gsl=gsl, msl=msl: nc.vector.tensor_tensor(
                                    out=msl, in0=bank[:rows, :], in1=gsl, op=ALU.mult), [bank, gl], [mix])
                            else:
                                k.op(dve, lambda bank=bank, gsl=gsl: nc.vector.tensor_tensor(
                                    out=tmpm[:rows, :], in0=bank[:rows, :], in1=gsl, op=ALU.mult), [bank, gl], [tmpm])
                                k.op(dve, lambda msl=msl: nc.vector.tensor_tensor(
                                    out=msl, in0=msl, in1=tmpm[:rows, :], op=ALU.add), [mix, tmpm], [mix])
                    transposes(mix, 0, 8, rows, mT, 0, 0, [0, 1])
                    for nb in range(2):
                        bank = ps[2 + nb]
                        for kc in range(8):
                            mm(bank[:rows, :], mT[:, kc, 0:rows], Wo[:, kc, nb * 512:(nb + 1) * 512], kc == 0, kc == 7,
                               [mT, Wo], [bank], sig=(kc == 7))
                    rs = rms_rstd(st, None, rows, None, from_psum2=(ps[2], ps[3]))
                    for nb in range(2):
                        bank = ps[2 + nb]
                        hs = hh[:rows, nb * 512:(nb + 1) * 512]
                        k.op(dve, lambda bank=bank, hs=hs, nb=nb: nc.vector.scalar_tensor_tensor(
                            out=hs, in0=bank[:rows, :], scalar=rs[:rows, 0:1], in1=G1b[g][:rows, nb * 512:(nb + 1) * 512],
                            op0=ALU.mult, op1=ALU.mult), [bank, rs, G1b[g]], [hh])
                    k.op(dve, lambda: nc.vector.tensor_tensor(out=xt[:rows, :], in0=xt[:rows, :], in1=hh[:rows, :],
                                                              op=ALU.add), [xt, hh], [xt])
                    k.dma(pool, XS[ro:ro + rows, :], xt[:rows, :], reads=[xt])
                    rs = rms_rstd(st, xt, rows, [xt])
                    k.op(dve, lambda: nc.vector.scalar_tensor_tensor(
                        out=hh[:rows, :], in0=xt[:rows, :], scalar=rs[:rows, 0:1], in1=A2b[g][:rows, :],
                        op0=ALU.mult, op1=ALU.mult), [xt, rs, A2b[g]], [hh])
                    k.op(dve, lambda: nc.vector.tensor_tensor(out=hh[:rows, :], in0=hh[:rows, :], in1=B2b[g][:rows, :],
                                                              op=ALU.add), [hh, B2b[g]], [hh])
                    transposes(hh, 0, 8, rows, h2T, 0, ro, [0, 1])
                k.barrier()

              tchunks = [(0, 512), (512, 512), (1024, 512), (1536, 512), (2048, 32)]
              with ExitStack() as ph2:
                  wst = [T(ph2, [128, 8, 128]) for _ in range(2)]
                  wbf = [T(ph2, [128, 8, 128], BF16) for _ in range(2)]
                  ugs = [T(ph2, [128, 512]) for _ in range(4)]
                  n = 0
                  for fc in range(22):
                      wf = wst[fc % 2]; wb = wbf[fc % 2]
                      k.dma(sp, wf[:], w_up[l][:, fc * 128:(fc + 1) * 128].rearrange("(kc p) n -> p kc n", p=128),
                            writes=[wf])
                      evac(wb[:], wf[:], [wf], [wb])
                      for (t0, tw) in tchunks:
                          bank = ps[4 + n % 4]; ug = ugs[n % 4]; n += 1
                          for kc in range(8):
                              mm(bank[:, 0:tw], wb[:, kc, :], h2T[:, kc, t0:t0 + tw], kc == 0, kc == 7, [wb, h2T], [bank], sig=(kc == 7))
                          evac(ug[:, 0:tw], bank[:, 0:tw], [bank], [ug])
                          r0 = t0 // 128
                          if tw == 512:
                              k.dma(pool, UG[fc * 128:(fc + 1) * 128, r0:r0 + 4, 2:130],
                                    ug[:, :].rearrange("p (a b) -> p a b", b=128), reads=[ug])
                              k.dma(pool, UGH_in[fc * 128:(fc + 1) * 128, r0 * 2:r0 * 2 + 8].rearrange(
                                  "p (a b) -> p a b", b=2), ug[:, :].rearrange("p (a b) -> p a b", b=128)[:, :, 126:128],
                                  reads=[ug])
                              if r0 == 12:
                                  k.dma(pool, o_cf[l, 0, fc * 128:(fc + 1) * 128, :], ug[:, 510:512], reads=[ug])
                          else:
                              k.dma(pool, UG[fc * 128:(fc + 1) * 128, 16, 2:34], ug[:, 0:32], reads=[ug])
                              k.dma(pool, o_cf[l, 1, fc * 128:(fc + 1) * 128, :], ug[:, 30:32], reads=[ug])
                  k.dma(pool, UG[:, 16, 0:2], scf_T[l], reads=[])
                  k.barrier()
                  k.op(pool, lambda: nc.gpsimd.collective_compute(
                      "AllGather", ALU.bypass, replica_groups=RG8, ins=[UGH_in[:, :]], outs=[UGH_all[:, :]]), [], [])
                  k.barrier()
                  gA = T(ph2, [128, 22, 8, 32]); gacc = T(ph2, [128, 22, 32]); gsh = T(ph2, [128, 22, 32])
                  for r in range(8):
                      k.dma(sp, gA[:, :, r, :], UGH_all.rearrange("(r fc p) n -> p fc r n", r=8, p=128)[:, :, r, :],
                            writes=[gA])
                  k.op(dve, lambda: nc.vector.memset(gsh[:], 0.0), [], [gsh])
                  k.op(dve, lambda: nc.vector.tensor_copy(out=gsh[:, :, 2:32], in_=gA[:, :, 7, 0:30]), [gA], [gsh])
                  k.op(dve, lambda: nc.vector.tensor_scalar(out=gacc[:], in0=gsh[:], scalar1=selB[:, 0:1], scalar2=None,
                                                            op0=ALU.mult), [gsh, selB], [gacc])
                  for r in range(8):
                      k.op(dve, lambda r=r: nc.vector.scalar_tensor_tensor(
                          out=gacc[:], in0=gA[:, :, r, :], scalar=selA[:, r:r + 1], in1=gacc[:], op0=ALU.mult,
                          op1=ALU.add), [gA, selA, gacc], [gacc])
                  for fc in range(22):
                      k.dma(pool, UG[fc * 128:(fc + 1) * 128, 0:16, 0:2],
                            gacc[:, fc, :].rearrange("p (a b) -> p a b", b=2), reads=[gacc])
                  k.barrier()

              with ExitStack() as ph2:
                  wst = [T(ph2, [128, 8, 128]) for _ in range(2)]
                  wbf = [T(ph2, [128, 8, 128], BF16) for _ in range(2)]
                  ugx = [T(ph2, [128, NRG, 130]) for _ in range(2)]
                  cvs = [T(ph2, [128, NRG, 128]) for _ in range(2)]
                  tmpc = T(ph2, [128, NRG, 128])
                  ats = [T(ph2, [128, 512], BF16) for _ in range(4)]
                  cwf = T(ph2, [128, 22, 3])
                  k.dma(sp, cwf[:], conv_ffn_wT[l], writes=[cwf])
                  n = 0
                  for fc in range(22):
                      wf = wst[fc % 2]; wb = wbf[fc % 2]; ux = ugx[fc % 2]; cv = cvs[fc % 2]
                      k.dma(sp, wf[:], w_up[l][:, DFF + fc * 128:DFF + (fc + 1) * 128].rearrange(
                          "(kc p) n -> p kc n", p=128), writes=[wf])
                      evac(wb[:], wf[:], [wf], [wb])
                      k.dma(sp, ux[:], UG[fc * 128:(fc + 1) * 128, :, :], writes=[ux])
                      k.op(dve, lambda ux=ux, cv=cv, fc=fc: nc.vector.tensor_scalar(
                          out=cv[:], in0=ux[:, :, 0:128], scalar1=cwf[:, fc, 0:1], scalar2=None, op0=ALU.mult),
                          [ux, cwf], [cv])
                      for s in (1, 2):
                          k.op(dve, lambda ux=ux, fc=fc, s=s: nc.vector.tensor_scalar(
                              out=tmpc[:], in0=ux[:, :, s:s + 128], scalar1=cwf[:, fc, s:s + 1], scalar2=None,
                              op0=ALU.mult), [ux, cwf], [tmpc])
                          k.op(dve, lambda cv=cv: nc.vector.tensor_tensor(out=cv[:], in0=cv[:], in1=tmpc[:],
                                                                          op=ALU.add), [cv, tmpc], [cv])
                      k.op(act, lambda cv=cv: nc.scalar.activation(out=cv[:], in_=cv[:], func=AF.Silu), [cv], [cv])
                      for (t0, tw) in tchunks:
                          bank = ps[4 + n % 4]; n += 1
                          for kc in range(8):
                              mm(bank[:, 0:tw], wb[:, kc, :], h2T[:, kc, t0:t0 + tw], kc == 0, kc == 7, [wb, h2T], [bank], sig=(kc == 7))
                          r0 = t0 // 128
                          if tw == 512:
                              cvv = cv[:, r0:r0 + 4, :].rearrange("p a b -> p (a b)")
                          else:
                              cvv = cv[:, 16, 0:32]
                          at_ = ats[n % 4]
                          k.op(dve, lambda bank=bank, cvv=cvv, at_=at_, tw=tw: nc.vector.tensor_tensor(
                              out=at_[:, 0:tw], in0=bank[:, 0:tw], in1=cvv, op=ALU.mult), [bank, cv], [at_])
                          k.dma(pool, AT[fc, :, t0:t0 + tw], at_[:, 0:tw], reads=[at_])
                  k.barrier()

            with ExitStack() as ph:
                wfs = [T(ph, [128, 1024]) for _ in range(2)]
                Wd = load_w_bf16(ph, w_down[l], 22, 1024, wfs, "wd")
                G2b = load_mod(ph, l, 5)
                atl = [T(ph, [128, 22, 128], BF16) for _ in range(2)]
                xts = [T(ph, [128, D]) for _ in range(2)]
                hh = T(ph, [128, D])
                st = (T(ph, [128, D]), T(ph, [128, 1]), T(ph, [128, 1]), T(ph, [128, 1]))
                ydst = XS if l == 0 else y_o
                for i in range(NRG):
                    rows, ro, g = rg_info(i)
                    at_ = atl[i % 2]; xt = xts[i % 2]
                    k.dma(sp, at_[:, :, 0:rows], AT[:, :, ro:ro + rows].rearrange("f p t -> p f t"), writes=[at_])
                    k.dma(sp, xt[:rows, :], XS[ro:ro + rows, :], writes=[xt])
                    for nb in range(2):
                        bank = ps[2 + nb]
                        for fc in range(22):
                            mm(bank[:rows, :], at_[:, fc, 0:rows], Wd[:, fc, nb * 512:(nb + 1) * 512], fc == 0, fc == 21,
                               [at_, Wd], [bank], sig=(fc == 21))
                    rs = rms_rstd(st, None, rows, None, from_psum2=(ps[2], ps[3]))
                    for nb in range(2):
                        bank = ps[2 + nb]
                        hs = hh[:rows, nb * 512:(nb + 1) * 512]
                        k.op(dve, lambda bank=bank, hs=hs, nb=nb: nc.vector.scalar_tensor_tensor(
                            out=hs, in0=bank[:rows, :], scalar=rs[:rows, 0:1], in1=G2b[g][:rows, nb * 512:(nb + 1) * 512],
                            op0=ALU.mult, op1=ALU.mult), [bank, rs, G2b[g]], [hh])
                    k.op(dve, lambda: nc.vector.tensor_tensor(out=xt[:rows, :], in0=xt[:rows, :], in1=hh[:rows, :],
                                                              op=ALU.add), [xt, hh], [xt])
                    k.dma(pool, ydst[ro:ro + rows, :], xt[:rows, :], reads=[xt])
                k.barrier()
    return nc


_PROG = {}


def _get_prog():
    if "nc" not in _PROG:
        _PROG["nc"] = build_program()
    return _PROG["nc"]


def _consts(c):
    f = np.float32
    q = np.arange(128)
    ident = np.eye(128, dtype=f)
    ut = (q[:, None] <= q[None, :]).astype(f)
    e127 = np.zeros((128, 128), f); e127[127, :] = 1.0
    r = np.arange(8)
    kp = r[None, :, None] * 128 + q[None, None, :]
    qp = c * 128 + q[:, None, None]
    negmask = np.where(kp < (qp // 64 + 1) * 64, 0.0, -1e30).astype(f)
    ks = r[None, :, None] * 128 + q[:, None, None]
    qq = c * 128 + q[None, None, :]
    foxband = np.where(ks <= qq, 0.0, -30000.0).astype(f)
    s32 = np.arange(32)
    foxnew = np.where(s32[:, None] <= s32[None, :], 0.0, -30000.0).astype(f)
    selA = np.zeros((128, 8), f)
    if c >= 1:
        selA[:, c - 1] = 1.0
    selB = np.full((128, 1), 1.0 if c == 0 else 0.0, f)
    oh = np.zeros((128, 8), f); oh[:, c] = 1.0
    return dict(ident=ident, ut=ut, e127=e127, negmask=negmask, foxband=foxband, foxnew=foxnew, selA=selA, selB=selB,
                oh=oh)


def kernel(x_prompt, x_sample, c_prompt, c_sample, cache_idx_k, cache_dsa_k, cache_dsa_v,
           cache_fox_k, cache_fox_v, cache_fox_logf, state_conv_mix, state_conv_ffn,
           w_ada, b_ada, norm_g, w_in, b_forget, conv_mix_w, w_branch, w_out, w_up,
           conv_ffn_w, w_down):
    f = np.float32
    A = lambda a: np.ascontiguousarray(np.asarray(a, dtype=f))
    x_prompt, x_sample, c_prompt, c_sample = A(x_prompt), A(x_sample), A(c_prompt), A(c_sample)
    shared = dict(
        w_ada=A(w_ada), b_ada=A(b_ada), norm_g=A(np.asarray(norm_g).reshape(2, 4 * D)), w_in=A(w_in),
        b_forget=A(b_forget), conv_mix_w=A(np.asarray(conv_mix_w).reshape(2, 1536)),
        w_branch=A(np.asarray(w_branch).reshape(2, 1536, D)), w_out=A(w_out), w_up=A(w_up),
        conv_ffn_wT=A(np.asarray(conv_ffn_w).reshape(2, 3, 22, 128).transpose(0, 3, 2, 1)), w_down=A(w_down))
    xt = x_prompt[0].reshape(128, 128, D)
    in_maps = []
    for c in range(NCORES):
        m = dict(shared)
        m["x_in"] = A(np.concatenate([xt[c::8].reshape(2048, D), x_sample[c]], axis=0))
        cv = np.stack([c_prompt[0], c_sample[c]], axis=0)
        m["cvecT"] = A(cv.reshape(2, 8, 128).transpose(2, 1, 0))
        m["kc_T"] = A(np.stack([np.concatenate([
            np.asarray(cache_dsa_k[l, c]).reshape(2048, 512).T, np.asarray(cache_fox_k[l, c]).reshape(2048, 512).T,
            np.asarray(cache_idx_k[l, c]).T], axis=0) for l in range(2)], axis=0))
        m["vc"] = A(np.stack([np.concatenate([
            np.asarray(cache_dsa_v[l, c]).reshape(2048, 512), np.asarray(cache_fox_v[l, c]).reshape(2048, 512)], axis=1)
            for l in range(2)], axis=0))
        m["lfc"] = A(np.asarray(cache_fox_logf)[:, c])
        m["scm"] = A(np.asarray(state_conv_mix)[:, c])
        m["scf_T"] = A(np.asarray(state_conv_ffn)[:, c].transpose(0, 2, 1))
        m.update(_consts(c))
        in_maps.append(m)
    nc = _get_prog()
    res = run_bass_kernel_spmd(nc, in_maps, core_ids=list(range(NCORES)))
    R = res.results

    def prompt_rows(name, l=None):
        parts = [np.asarray(R[c][name]) for c in range(NCORES)]
        if l is None:
            arr = np.stack([p[0:2048].reshape(16, 128, -1) for p in parts], axis=1)
        else:
            arr = np.stack([p[l, 0:2048].reshape(16, 128, -1) for p in parts], axis=1)
        return arr.reshape(16384, -1)

    def sample_rows(name, l=None):
        if l is None:
            return np.stack([np.asarray(R[c][name])[2048:2080] for c in range(NCORES)], axis=0)
        return np.stack([np.asarray(R[c][name])[l, 2048:2080] for c in range(NCORES)], axis=0)

    y_p = prompt_rows("y").reshape(1, 16384, D).astype(f)
    y_s = sample_rows("y").astype(f)

    def pst(name, shp):
        return np.stack([prompt_rows(name, l) for l in range(2)], axis=0).reshape((2, 1, 16384) + shp).astype(f)

    def sst(name, shp):
        return np.stack([sample_rows(name, l) for l in range(2)], axis=0).reshape((2, 8, 32) + shp).astype(f)

    p_cm = np.asarray(R[7]["o_cm"])[:, 0].reshape(2, 1, 2, 512).astype(f)
    s_cm = np.stack([np.asarray(R[c]["o_cm"])[:, 1] for c in range(NCORES)], axis=1).astype(f)
    p_cf = np.asarray(R[7]["o_cf"])[:, 0].transpose(0, 2, 1).reshape(2, 1, 2, DFF).astype(f)
    s_cf = np.stack([np.asarray(R[c]["o_cf"])[:, 1].transpose(0, 2, 1) for c in range(NCORES)], axis=1).astype(f)
    return (y_p, y_s,
            pst("o_idx", (64,)), pst("o_dk", (8, 64)), pst("o_dv", (8, 64)), pst("o_fk", (8, 64)), pst("o_fv", (8, 64)),
            pst("o_lf", (8,)), np.ascontiguousarray(p_cm), np.ascontiguousarray(p_cf),
            sst("o_idx", (64,)), sst("o_dk", (8, 64)), sst("o_dv", (8, 64)), sst("o_fk", (8, 64)), sst("o_fv", (8, 64)),
            sst("o_lf", (8,)), np.ascontiguousarray(s_cm), np.ascontiguousarray(s_cf))
```
